# Optimizing a Trainium2 kernel written in Bass

```python
import math
import jax, jax.numpy as jnp
from jax import lax
import numpy as np

D_MODEL = 2048
BATCH = 1
SEQ = 8192
DEPTH = 1

CHUNK = 64
N_META = 16
META_PAD = (-N_META) % CHUNK
Q_BLOCK = 128
PAD_CHUNK = 2 ** 30
M_HEADS = 4
M_DQK = 128
M_DV = 256
M_QK = M_HEADS * M_DQK
M_V = M_HEADS * M_DV
CONV_W = 4
A_HEADS = 8
A_DH = 64
A_DV = 2 * A_DH
A_QK = A_HEADS * 2 * A_DH
A_V = A_HEADS * A_DV
ROT_DIM = A_DH // 4
ROPE_THETA = 500000.0
D_FF = 5632
EPS = 1e-6
IN_SPLITS = (M_QK, M_QK, M_V, M_V, M_HEADS, M_HEADS, A_QK, A_QK, A_V, D_MODEL, D_MODEL)
N_IN = sum(IN_SPLITS)
F_GATE_OFFSET = 2 * M_QK + 2 * M_V + M_HEADS

kernel_name = "hybrid_mlstm_diffattn_macaron_block"


def rmsnorm(x, g):
    xf = x.astype(jnp.float32)
    xf = xf * lax.rsqrt(jnp.mean(xf * xf, axis=-1, keepdims=True) + EPS)
    return xf.astype(x.dtype) * g


def head_rmsnorm(t, g):
    H, d = t.shape[-2], t.shape[-1]
    return rmsnorm(t, g.reshape(H, d))


def swiglu(x, w_gate, w_up, w_down):
    return (jax.nn.silu(x @ w_gate) * (x @ w_up)) @ w_down


def causal_depthwise_conv(x, w, b):
    K, C = w.shape
    y = lax.conv_general_dilated(x, w[:, None, :].astype(x.dtype), window_strides=(1,),
                                 padding=[(K - 1, 0)], dimension_numbers=('NWC', 'WIO', 'NWC'),
                                 feature_group_count=C)
    return y + b


def chunk_ids(L):
    p = jnp.arange(L, dtype=jnp.int32)
    return jnp.where(p < N_META, 0, 1 + (p - N_META) // CHUNK).astype(jnp.int32)


def rope_tables(L):
    inv_freq = ROPE_THETA ** (-jnp.arange(0, ROT_DIM, 2, dtype=jnp.float32) / ROT_DIM)
    ang = jnp.arange(L, dtype=jnp.float32)[:, None] * inv_freq[None, :]
    return jnp.cos(ang), jnp.sin(ang)


def partial_rope(x, cos, sin):
    half = ROT_DIM // 2
    x1 = x[..., :half].astype(jnp.float32)
    x2 = x[..., half:ROT_DIM].astype(jnp.float32)
    r1 = x1 * cos - x2 * sin
    r2 = x2 * cos + x1 * sin
    return jnp.concatenate([r1.astype(x.dtype), r2.astype(x.dtype), x[..., ROT_DIM:]], axis=-1)


def mlstm_chunkwise(q, k, v, log_i, log_f):
    out_dtype = v.dtype
    f32 = jnp.float32
    q, k, v = q.astype(f32), k.astype(f32), v.astype(f32)
    log_i, log_f = log_i.astype(f32), log_f.astype(f32)
    B, H, L, dk = q.shape
    dv = v.shape[-1]
    pw = ((0, 0), (0, 0), (META_PAD, 0))
    q = jnp.pad(q, pw + ((0, 0),))
    k = jnp.pad(k, pw + ((0, 0),))
    v = jnp.pad(v, pw + ((0, 0),))
    log_i = jnp.pad(log_i, pw, constant_values=-jnp.inf)
    log_f = jnp.pad(log_f, pw)
    Lm = L + META_PAD
    nc = Lm // CHUNK
    to_chunks = lambda t: jnp.moveaxis(t.reshape(B, H, nc, CHUNK, *t.shape[3:]), 2, 0)
    qc, kc, vc, lic = to_chunks(q), to_chunks(k), to_chunks(v), to_chunks(log_i)
    bc = jnp.cumsum(to_chunks(log_f), axis=-1)
    tril = jnp.tril(jnp.ones((CHUNK, CHUNK), dtype=bool))

    def step(carry, inp):
        C, n, m = carry
        qi, ki, vi, li, bi = inp
        g = bi[..., -1]
        logD = jnp.where(tril, bi[..., :, None] - bi[..., None, :] + li[..., None, :], -jnp.inf)
        m_inter = bi + m[..., None]
        m_row = jnp.maximum(m_inter, jnp.max(logD, axis=-1))
        w_inter = jnp.exp(m_inter - m_row)
        s = jnp.einsum('bhid,bhjd->bhij', qi, ki) * jnp.exp(logD - m_row[..., None])
        num = w_inter[..., None] * jnp.einsum('bhid,bhde->bhie', qi, C) + jnp.einsum('bhij,bhje->bhie', s, vi)
        den = w_inter * jnp.einsum('bhid,bhd->bhi', qi, n) + jnp.sum(s, axis=-1)
        h = num / jnp.maximum(jnp.abs(den), jnp.exp(-m_row))[..., None]
        log_w = g[..., None] - bi + li
        m_new = jnp.maximum(g + m, jnp.max(log_w, axis=-1))
        wk = jnp.exp(log_w - m_new[..., None])
        decay = jnp.exp(g + m - m_new)
        C_new = decay[..., None, None] * C + jnp.einsum('bhj,bhjd,bhje->bhde', wk, ki, vi)
        n_new = decay[..., None] * n + jnp.einsum('bhj,bhjd->bhd', wk, ki)
        return (C_new, n_new, m_new), h

    init = (jnp.zeros((B, H, dk, dv), f32), jnp.zeros((B, H, dk), f32), jnp.zeros((B, H), f32))
    _, hs = lax.scan(step, init, (qc, kc, vc, lic, bc))
    hs = jnp.moveaxis(hs, 0, 2).reshape(B, H, Lm, dv)[:, :, META_PAD:]
    return hs.astype(out_dtype)


def diff_attention(q, k, v, lam, cid):
    B, H, _, L, dh = q.shape
    Lp = -(-L // Q_BLOCK) * Q_BLOCK
    pad = Lp - L
    q = jnp.pad(q, ((0, 0), (0, 0), (0, 0), (0, pad), (0, 0)))
    k = jnp.pad(k, ((0, 0), (0, 0), (0, 0), (0, pad), (0, 0)))
    v = jnp.pad(v, ((0, 0), (0, 0), (0, pad), (0, 0)))
    cidp = jnp.concatenate([cid, jnp.full((pad,), PAD_CHUNK, cid.dtype)])
    nqb = Lp // Q_BLOCK
    qb = q.reshape(B, H, 2, nqb, Q_BLOCK, dh).transpose(3, 0, 1, 2, 4, 5)
    cqb = cidp.reshape(nqb, Q_BLOCK)
    scale = dh ** -0.5

    def block(args):
        qi, ci = args
        s = jnp.einsum('bhmqd,bhmkd->bhmqk', qi, k).astype(jnp.float32) * scale
        vis = cidp[None, :] <= ci[:, None]
        p = jax.nn.softmax(jnp.where(vis, s, -jnp.inf), axis=-1)
        a = p[:, :, 0] - lam * p[:, :, 1]
        return jnp.einsum('bhqk,bhke->bhqe', a.astype(v.dtype), v)

    o = lax.map(block, (qb, cqb))
    return o.transpose(1, 2, 0, 3, 4).reshape(B, H, Lp, -1)[:, :, :L]


def lambda_init_fn(layer):
    return 0.8 - 0.6 * math.exp(-0.3 * layer)


def hybrid_mixer(u, w_in, b_in, conv_w, conv_b, m_norm_g, m_w_branch, lq1, lk1, lq2, lk2,
                 a_norm_g, a_w_branch, w_out, cos, sin, cid, lam_init):
    B, L, _ = u.shape
    z = u @ w_in + b_in
    split_idx = [int(i) for i in np.cumsum(IN_SPLITS)[:-1]]
    mq, mk, mv, mo, mi, mf, aq, ak, av, gm, ga = jnp.split(z, split_idx, axis=-1)
    qk = jax.nn.silu(causal_depthwise_conv(jnp.concatenate([mq, mk], axis=-1), conv_w, conv_b))
    mq, mk = jnp.split(qk, 2, axis=-1)
    q = mq.reshape(B, L, M_HEADS, M_DQK).transpose(0, 2, 1, 3)
    k = mk.reshape(B, L, M_HEADS, M_DQK).transpose(0, 2, 1, 3) * (M_DQK ** -0.5)
    v = mv.reshape(B, L, M_HEADS, M_DV).transpose(0, 2, 1, 3)
    log_i = mi.transpose(0, 2, 1)
    log_f = jax.nn.log_sigmoid(mf.astype(jnp.float32)).transpose(0, 2, 1)
    hm = mlstm_chunkwise(q, k, v, log_i, log_f)
    hm = head_rmsnorm(hm.transpose(0, 2, 1, 3), m_norm_g).reshape(B, L, M_V)
    br_m = (jax.nn.sigmoid(mo) * hm) @ m_w_branch
    aq = partial_rope(aq.reshape(B, L, A_HEADS, 2, A_DH).transpose(0, 2, 3, 1, 4), cos, sin)
    ak = partial_rope(ak.reshape(B, L, A_HEADS, 2, A_DH).transpose(0, 2, 3, 1, 4), cos, sin)
    av = av.reshape(B, L, A_HEADS, A_DV).transpose(0, 2, 1, 3)
    f32 = jnp.float32
    lam = (jnp.exp(jnp.sum(lq1.astype(f32) * lk1.astype(f32)))
           - jnp.exp(jnp.sum(lq2.astype(f32) * lk2.astype(f32))) + lam_init)
    ha = diff_attention(aq, ak, av, lam, cid)
    ha = (head_rmsnorm(ha.transpose(0, 2, 1, 3), a_norm_g) * (1.0 - lam_init)).reshape(B, L, A_V)
    br_a = ha @ a_w_branch
    y = jax.nn.sigmoid(gm) * br_m + jax.nn.sigmoid(ga) * br_a
    return y @ w_out


def setup_inputs(seed: int = 0) -> dict:
    key = jax.random.key(seed)
    ks = jax.random.split(key, 32)
    f32 = jnp.float32
    nrm = lambda k, shape, scale: jax.random.normal(k, shape, f32) * scale
    gain = lambda k, shape: 1.0 + 0.02 * jax.random.normal(k, shape, f32)
    b_in = nrm(ks[11], (DEPTH, N_IN), 0.02)
    f_bias = jnp.linspace(3.0, 6.0, M_HEADS, dtype=f32)
    b_in = b_in.at[:, F_GATE_OFFSET:F_GATE_OFFSET + M_HEADS].add(f_bias)
    return {
        "x": nrm(ks[0], (BATCH, SEQ, D_MODEL), 1.0),
        "meta": nrm(ks[1], (N_META, D_MODEL), 1.0),
        "ffn1_pre_g": gain(ks[2], (DEPTH, D_MODEL)),
        "ffn1_post_g": gain(ks[3], (DEPTH, D_MODEL)),
        "ffn1_w_gate": nrm(ks[4], (DEPTH, D_MODEL, D_FF), D_MODEL ** -0.5),
        "ffn1_w_up": nrm(ks[5], (DEPTH, D_MODEL, D_FF), D_MODEL ** -0.5),
        "ffn1_w_down": nrm(ks[6], (DEPTH, D_FF, D_MODEL), D_FF ** -0.5),
        "mix_pre_g": gain(ks[7], (DEPTH, D_MODEL)),
        "mix_post_g": gain(ks[8], (DEPTH, D_MODEL)),
        "w_in": nrm(ks[9], (DEPTH, D_MODEL, N_IN), D_MODEL ** -0.5),
        "b_in": b_in,
        "m_conv_w": nrm(ks[12], (DEPTH, CONV_W, 2 * M_QK), CONV_W ** -0.5),
        "m_conv_b": nrm(ks[13], (DEPTH, 2 * M_QK), 0.02),
        "m_norm_g": gain(ks[14], (DEPTH, M_V)),
        "m_w_branch": nrm(ks[15], (DEPTH, M_V, D_MODEL), M_V ** -0.5),
        "a_lambda_q1": nrm(ks[16], (DEPTH, A_DH), 0.1),
        "a_lambda_k1": nrm(ks[17], (DEPTH, A_DH), 0.1),
        "a_lambda_q2": nrm(ks[18], (DEPTH, A_DH), 0.1),
        "a_lambda_k2": nrm(ks[19], (DEPTH, A_DH), 0.1),
        "a_norm_g": gain(ks[20], (DEPTH, A_V)),
        "a_w_branch": nrm(ks[21], (DEPTH, A_V, D_MODEL), A_V ** -0.5),
        "w_out": nrm(ks[22], (DEPTH, D_MODEL, D_MODEL), D_MODEL ** -0.5),
        "ffn2_pre_g": gain(ks[23], (DEPTH, D_MODEL)),
        "ffn2_post_g": gain(ks[24], (DEPTH, D_MODEL)),
        "ffn2_w_gate": nrm(ks[25], (DEPTH, D_MODEL, D_FF), D_MODEL ** -0.5),
        "ffn2_w_up": nrm(ks[26], (DEPTH, D_MODEL, D_FF), D_MODEL ** -0.5),
        "ffn2_w_down": nrm(ks[27], (DEPTH, D_FF, D_MODEL), D_FF ** -0.5),
    }


def reference(x, meta, ffn1_pre_g, ffn1_post_g, ffn1_w_gate, ffn1_w_up, ffn1_w_down,
              mix_pre_g, mix_post_g, w_in, b_in, m_conv_w, m_conv_b, m_norm_g, m_w_branch,
              a_lambda_q1, a_lambda_k1, a_lambda_q2, a_lambda_k2, a_norm_g, a_w_branch, w_out,
              ffn2_pre_g, ffn2_post_g, ffn2_w_gate, ffn2_w_up, ffn2_w_down):
    B = x.shape[0]
    h = jnp.concatenate([jnp.broadcast_to(meta[None].astype(x.dtype), (B, N_META, D_MODEL)), x], axis=1)
    L = h.shape[1]
    cid = chunk_ids(L)
    cos, sin = rope_tables(L)
    for l in range(DEPTH):
        u = rmsnorm(h, ffn1_pre_g[l])
        h = h + 0.5 * rmsnorm(swiglu(u, ffn1_w_gate[l], ffn1_w_up[l], ffn1_w_down[l]), ffn1_post_g[l])
        u = rmsnorm(h, mix_pre_g[l])
        y = hybrid_mixer(u, w_in[l], b_in[l], m_conv_w[l], m_conv_b[l], m_norm_g[l], m_w_branch[l],
                         a_lambda_q1[l], a_lambda_k1[l], a_lambda_q2[l], a_lambda_k2[l],
                         a_norm_g[l], a_w_branch[l], w_out[l], cos, sin, cid, lambda_init_fn(l))
        h = h + rmsnorm(y, mix_post_g[l])
        u = rmsnorm(h, ffn2_pre_g[l])
        h = h + 0.5 * rmsnorm(swiglu(u, ffn2_w_gate[l], ffn2_w_up[l], ffn2_w_down[l]), ffn2_post_g[l])
    return h[:, N_META:]
```

```python
import contextlib
import math
import numpy as np
import ml_dtypes
import concourse.bass as bass
import concourse.mybir as mybir
from concourse.bass_utils import run_bass_kernel_spmd

F32 = mybir.dt.float32
BF16 = mybir.dt.bfloat16
AF = mybir.ActivationFunctionType
ALU = mybir.AluOpType
AX = mybir.AxisListType

NCORES = 8
D = 2048
NDC = 16
DFF = 5632
NF = 44
SEQ = 8192
NMETA = 16
TPC = SEQ // NCORES
EPS = 1e-6
N_IN = 10248
LAM_INIT = 0.8 - 0.6 * math.exp(-0.3 * 0)
LK = SEQ + NMETA
ROPE_THETA = 500000.0


class Buf:
    __slots__ = ("w", "rs", "name")

    def __init__(self, name=""):
        self.w = None
        self.rs = []
        self.name = name


class Op:
    __slots__ = ("eng", "fn", "deps", "sem", "val", "need", "idx", "isdma", "inc")


class Prog:
    ENGS = ("pe", "act", "dve", "pool", "sp")

    def __init__(self, nc):
        self.nc = nc
        self.ops = {e: [] for e in self.ENGS}
        self.dma_cnt = {}
        self.nkeys = 0
        self.last_dma = {}
        self.bar = {}
        self.dyn_prologue = None
        self.nbar = 0

    def newkey(self, pfx="k"):
        self.nkeys += 1
        return "k%d" % self.nkeys

    def add(self, eng, fn, reads=(), writes=()):
        op = Op()
        op.eng = eng
        op.fn = fn
        op.sem = eng
        op.val = None
        op.need = False
        op.isdma = False
        op.idx = len(self.ops[eng])
        op.inc = 1
        deps = []
        if self.bar.get(eng):
            deps.extend(self.bar.pop(eng))
        for b in reads:
            if b.w is not None:
                deps.append(b.w)
        for b in writes:
            if b.w is not None:
                deps.append(b.w)
            deps.extend(b.rs)
        op.deps = deps
        for b in reads:
            b.rs.append(op)
        for b in writes:
            b.w = op
            b.rs = []
        self.ops[eng].append(op)
        return op

    def dma(self, eng, out, in_, reads=(), writes=(), sem=None, **kw):
        return self.dma_fn(eng, lambda e: e.dma_start(out=out, in_=in_, **kw), reads, writes, sem)

    def dma_fn(self, eng, fn, reads=(), writes=(), sem=None, inc=16):
        if sem is None:
            sem = self.newkey("d")
        op = self.add(eng, fn, reads, writes)
        op.isdma = True
        op.sem = sem
        op.inc = inc
        self.dma_cnt[sem] = self.dma_cnt.get(sem, 0) + inc
        op.val = self.dma_cnt[sem]
        op.need = True
        self.last_dma[sem] = op
        return op

    def barrier(self):
        deps = []
        for e in self.ENGS:
            for op in reversed(self.ops[e]):
                if not op.isdma:
                    deps.append(op)
                    break
        deps.extend(self.last_dma.values())
        src = self.nc.dram_tensor("bar_src%d" % self.nbar, [1, 16], F32).ap()
        dst = self.nc.dram_tensor("bar_dst%d" % self.nbar, [1, 16], F32).ap()
        self.nbar += 1
        op = self.dma("sp", dst, src, sem="bar")
        op.deps.extend(deps)
        self.bar = {e: [op] for e in self.ENGS}
        self.nkeys = 0

    def emit(self, final_waits=()):
        nc = self.nc
        plan = {e: [] for e in self.ENGS}
        for e in self.ENGS:
            seen = {}
            for op in self.ops[e]:
                best = {}
                for d in op.deps:
                    if d.isdma:
                        key = ("dma", d.sem)
                        rank = d.val
                    else:
                        if d.eng == "pe" and e == "pe":
                            continue
                        key = ("eng", d.eng)
                        rank = d.idx
                    if seen.get(key, -1) >= rank:
                        continue
                    if key not in best or best[key][0] < rank:
                        best[key] = (rank, d)
                ws = []
                for key, (rank, d) in best.items():
                    seen[key] = rank
                    d.need = True
                    ws.append(d)
                plan[e].append(ws)
        for e in self.ENGS:
            c = 0
            for op in self.ops[e]:
                if not op.isdma and op.need:
                    c += 1
                    op.val = c
        keys = list(self.ENGS) + sorted(self.dma_cnt.keys())
        with contextlib.ExitStack() as st:
            sems = {k: st.enter_context(nc.semaphore("s_" + str(k))) for k in keys}
            block = st.enter_context(nc.Block())

            def run(e, eng):
                if e == DYNQ and self.dyn_prologue is not None:
                    self.dyn_prologue(eng)
                for op, ws in zip(self.ops[e], plan[e]):
                    for d in ws:
                        eng.wait_ge(sems[d.sem], d.val)
                    ins = op.fn(eng)
                    if op.isdma:
                        if op.inc == 16:
                            ins.then_inc(sems[op.sem], 16)
                        else:
                            ins.then_inc(sems[op.sem])
                    elif op.need:
                        ins.then_inc(sems[op.sem], 1)
                if e == "sp":
                    for k in final_waits:
                        eng.wait_ge(sems[k], self.dma_cnt[k])

            @block.tensor
            def _(eng):
                run("pe", eng)

            @block.scalar
            def _(eng):
                run("act", eng)

            @block.vector
            def _(eng):
                run("dve", eng)

            @block.gpsimd
            def _(eng):
                run("pool", eng)

            @block.sync
            def _(eng):
                run("sp", eng)


_PID = {}


def pid_of(e):
    key = id(e)
    if key not in _PID:
        _PID[key] = (e, e.snap(e.partition_id()))
    return _PID[key][1]


DYNQ = "act"
DYN_DEFS = [("s_mq", lambda p: p // 2), ("s_mk", lambda p: p // 2 + 4), ("s_mv", lambda p: p + 8), ("s_aq", lambda p: p + 16),
            ("s_ak", lambda p: p + 24), ("s_av", lambda p: p + 32), ("g0", lambda p: p // 2), ("g4", lambda p: p // 2 + 4),
            ("c0", lambda p: p * TPC), ("c512", lambda p: p * TPC + 512)]


def dyn_prologue(e):
    pid_of(e)


def dyn(e, name, fn):
    return fn(pid_of(e))


class Ring:
    def __init__(self, K, name, shape, dt, n):
        self.items = []
        for i in range(n):
            t = K.sb("%s%d" % (name, i), shape, dt)
            self.items.append((t, Buf("%s%d" % (name, i)), K.P.newkey(name)))
        self.i = 0

    def next(self):
        it = self.items[self.i % len(self.items)]
        self.i += 1
        return it


class K:
    def __init__(self):
        self.nc = bass.Bass("TRN2", target_bir_lowering=False)
        self.P = Prog(self.nc)
        self.st = contextlib.ExitStack()
        self.outkeys = []
        self.npsum = 0
        self.pfx = ""
        self.fused = False

    def begin_phase(self, pfx):
        self.pfx = pfx
        self.st = contextlib.ExitStack()

    def end_phase(self):
        self.P.barrier()
        self.st.close()

    def sb(self, name, shape, dt):
        return self.st.enter_context(self.nc.sbuf_tensor(self.pfx + name, list(shape), dt))

    def psum(self, name, shape=(128, 512), dt=F32):
        return self.st.enter_context(self.nc.psum_tensor(self.pfx + name, list(shape), dt))

    def din(self, name, shape, dt=F32):
        return self.nc.dram_tensor(self.pfx + name, list(shape), dt, kind="ExternalInput").ap()

    def dout(self, name, shape, dt=F32):
        return self.nc.dram_tensor(self.pfx + name, list(shape), dt, kind="ExternalOutput").ap()

    def dint(self, name, shape, dt=F32):
        return self.nc.dram_tensor(name, list(shape), dt).ap()

    def const_load(self, name, dram_ap, shape, dt=F32, eng="sp"):
        t = self.sb(name, shape, dt)
        b = Buf(name)
        self.P.dma(eng, t[:], dram_ap, writes=[b])
        return t, b

    def out_dma(self, out_ap, in_ap, reads, key, eng="sp"):
        if key not in self.outkeys:
            self.outkeys.append(key)
        self.P.dma(eng, out_ap, in_ap, reads=reads, sem=key)

    def finish(self):
        if self.fused:
            return self.nc
        self.P.emit(final_waits=self.outkeys)
        self.st.close()
        return self.nc

    def finish_fused(self):
        self.P.emit(final_waits=self.outkeys)
        return self.nc


class Common:
    def __init__(self, k, W):
        self.k = k
        P = k.P
        self.W = W
        self.ones = k.sb("ones_bf", [128, 128], BF16)
        self.b_ones = Buf("ones")
        P.add("pool", lambda e: e.memset(self.ones[:], 1.0), writes=[self.b_ones])
        self.sq = Ring(k, "sq", [128, 512], BF16, 3)
        self.ps_stat = k.psum("ps_stat")
        self.b_stat = Buf("ps_stat")

    def rstd(self, chunks, n, Dn, out_t, out_b):
        P = self.k.P
        nchunks = len(chunks)
        for i, (ap, bufs) in enumerate(chunks):
            sq, sqb, _ = self.sq.next()
            eng = "act" if i % 2 == 0 else "dve"
            if eng == "act":
                P.add("act", lambda e, sq=sq, ap=ap: e.activation(sq[:, :n], ap, AF.Square), reads=bufs, writes=[sqb])
            else:
                P.add("dve", lambda e, sq=sq, ap=ap: e.tensor_tensor(sq[:, :n], ap, ap, ALU.mult), reads=bufs, writes=[sqb])
            P.add("pe", lambda e, sq=sq, i=i: e.matmul(self.ps_stat[:, :n], self.ones[:], sq[:, :n], start=(i == 0), stop=(i == nchunks - 1)),
                  reads=[sqb, self.b_ones], writes=[self.b_stat])
        P.add("act", lambda e: e.activation(out_t[:, :n], self.ps_stat[:, :n], AF.Sqrt, bias=EPS, scale=1.0 / Dn),
              reads=[self.b_stat], writes=[out_b])
        P.add("dve", lambda e: e.reciprocal(out_t[:, :n], out_t[:, :n]), reads=[out_b], writes=[out_b])


class FFN:
    def __init__(self, k, cm, W):
        self.k = k
        self.cm = cm
        self.W = W
        self.wg = Ring(k, "wg", [128, NDC * 128], BF16, 3)
        self.wu = Ring(k, "wu", [128, NDC * 128], BF16, 3)
        self.wd = Ring(k, "wd", [128, NF * 128], BF16, 2)
        self.uT = k.sb("uT", [128, NDC, W], BF16)
        self.b_uT = Buf("uT")
        self.hid = k.sb("hid", [128, NF, W], BF16)
        self.b_hid = [Buf("hid%d" % f) for f in range(NF)]
        self.yT = k.sb("yT", [128, NDC, W], F32)
        self.b_yT = [Buf("yT%d" % d) for d in range(NDC)]
        self.sg = Ring(k, "sg", [128, 512], F32, 2)
        self.rs = k.sb("rstd_a", [128, W], F32)
        self.b_rs = Buf("rstd_a")
        self.rs2 = k.sb("rstd_b", [128, W], F32)
        self.b_rs2 = Buf("rstd_b")
        self.psG = [(k.psum("psG%d" % i), Buf("psG%d" % i)) for i in range(2)]
        self.psU = [(k.psum("psU%d" % i), Buf("psU%d" % i)) for i in range(2)]
        self.psY = [(k.psum("psY%d" % i), Buf("psY%d" % i)) for i in range(2)]
        self.cnt = 0

    def prenorm(self, hT, b_h, g_t, b_g, segs):
        P = self.k.P
        for (lo, n) in segs:
            self.cm.rstd([(hT[:, dc, lo:lo + n], [b_h]) for dc in range(NDC)], n, D, self.rs, self.b_rs)
            for dc in range(NDC):
                eng = "dve" if dc % 4 != 3 else "pool"
                if eng == "dve":
                    P.add("dve", lambda e, dc=dc, lo=lo, n=n: e.scalar_tensor_tensor(
                        self.uT[:, dc, lo:lo + n], hT[:, dc, lo:lo + n], g_t[:, dc:dc + 1], self.rs[:, :n], ALU.mult, ALU.mult),
                        reads=[b_h, b_g, self.b_rs], writes=[self.b_uT])
                else:
                    P.add("dve", lambda e, dc=dc, lo=lo, n=n: e.scalar_tensor_tensor(
                        self.uT[:, dc, lo:lo + n], hT[:, dc, lo:lo + n], g_t[:, dc:dc + 1], self.rs[:, :n], ALU.mult, ALU.mult),
                        reads=[b_h, b_g, self.b_rs], writes=[self.b_uT])

    def run(self, hT, b_h, segs, pre_g, post_g, wg_d, wu_d, wd_d):
        k, P = self.k, self.k.P
        self.prenorm(hT, b_h, pre_g[0], pre_g[1], segs)
        R = 3
        loads = {}

        def load_gu(f):
            tg, bg, kg = self.wg.next()
            tu, bu, ku = self.wu.next()
            P.dma("pool", tg[:], wg_d[f], writes=[bg], sem=kg)
            P.dma("pool", tu[:], wu_d[f], writes=[bu], sem=ku)
            loads[f] = (tg, bg, tu, bu)

        for f in range(min(R, NF)):
            load_gu(f)
        for f in range(NF):
            tg, bg, tu, bu = loads.pop(f)
            for (lo, n) in segs:
                psg, bpg = self.psG[self.cnt % 2]
                psu, bpu = self.psU[self.cnt % 2]
                self.cnt += 1
                for dc in range(NDC):
                    P.add("pe", lambda e, dc=dc, psg=psg, tg=tg, lo=lo, n=n: e.matmul(
                        psg[:, :n], tg[:, dc * 128:(dc + 1) * 128], self.uT[:, dc, lo:lo + n], start=(dc == 0), stop=(dc == NDC - 1)),
                        reads=[bg, self.b_uT], writes=[bpg])
                for dc in range(NDC):
                    P.add("pe", lambda e, dc=dc, psu=psu, tu=tu, lo=lo, n=n: e.matmul(
                        psu[:, :n], tu[:, dc * 128:(dc + 1) * 128], self.uT[:, dc, lo:lo + n], start=(dc == 0), stop=(dc == NDC - 1)),
                        reads=[bu, self.b_uT], writes=[bpu])
                sg, bsg, _ = self.sg.next()
                P.add("act", lambda e, sg=sg, psg=psg, n=n: e.activation(sg[:, :n], psg[:, :n], AF.Silu), reads=[bpg], writes=[bsg])
                P.add("dve", lambda e, sg=sg, psu=psu, f=f, lo=lo, n=n: e.tensor_tensor(
                    self.hid[:, f, lo:lo + n], sg[:, :n], psu[:, :n], ALU.mult), reads=[bsg, bpu], writes=[self.b_hid[f]])
            if f + R < NF:
                load_gu(f + R)
        dl = {}

        def load_d(dc):
            t, b, kk = self.wd.next()
            P.dma("pool", t[:], wd_d[dc], writes=[b], sem=kk)
            dl[dc] = (t, b)

        for dc in range(min(2, NDC)):
            load_d(dc)
        for dc in range(NDC):
            t, b = dl.pop(dc)
            for (lo, n) in segs:
                psy, bpy = self.psY[self.cnt % 2]
                self.cnt += 1
                for f in range(NF):
                    P.add("pe", lambda e, f=f, psy=psy, t=t, lo=lo, n=n: e.matmul(
                        psy[:, :n], t[:, f * 128:(f + 1) * 128], self.hid[:, f, lo:lo + n], start=(f == 0), stop=(f == NF - 1)),
                        reads=[b, self.b_hid[f]], writes=[bpy])
                P.add("act", lambda e, psy=psy, dc=dc, lo=lo, n=n: e.activation(self.yT[:, dc, lo:lo + n], psy[:, :n], AF.Copy),
                      reads=[bpy], writes=[self.b_yT[dc]])
            if dc + 2 < NDC:
                load_d(dc + 2)
        self.postnorm_residual(hT, b_h, self.yT, self.b_yT, segs, post_g, 0.5)

    def postnorm_residual(self, hT, b_h, yT, b_yT, segs, post_g, coef):
        P = self.k.P
        for (lo, n) in segs:
            self.cm.rstd([(yT[:, dc, lo:lo + n], [b_yT[dc]]) for dc in range(NDC)], n, D, self.rs2, self.b_rs2)
            for dc in range(NDC):
                P.add("dve", lambda e, dc=dc, lo=lo, n=n: e.scalar_tensor_tensor(
                    yT[:, dc, lo:lo + n], yT[:, dc, lo:lo + n], post_g[0][:, dc:dc + 1], self.rs2[:, :n], ALU.mult, ALU.mult),
                    reads=[post_g[1], self.b_rs2, b_yT[dc]], writes=[b_yT[dc]])
                P.add("dve", lambda e, dc=dc, lo=lo, n=n: e.scalar_tensor_tensor(
                    hT[:, dc, lo:lo + n], yT[:, dc, lo:lo + n], coef, hT[:, dc, lo:lo + n], ALU.mult, ALU.add),
                    reads=[b_yT[dc], b_h], writes=[b_h])


NTA = TPC + NMETA
NZC = 80


XSLOT = {}
for _h in range(4):
    XSLOT[_h] = 2 * _h
    XSLOT[4 + _h] = 2 * _h + 1
for _c in range(8):
    XSLOT[8 + _c] = 8 + 4 * _c
    XSLOT[24 + _c] = 8 + 4 * _c + 1
    XSLOT[32 + _c] = 8 + 4 * _c + 2
    XSLOT[40 + _c] = 8 + 4 * _c + 3
LSLOT = {ci: ci - 16 for ci in range(16, 24)}
LSLOT.update({ci: ci - 40 for ci in range(48, 80)})
NXS = 40


def build_A(k=None, fz=None):
    if k is None:
        k = K()
    P = k.P
    W = 528
    xT = k.din("xT", [NDC, 128, NTA])
    g1 = k.din("g_pre1", [128, NDC])
    g2 = k.din("g_post1", [128, NDC])
    g3 = k.din("g_mix", [128, NDC])
    wg_d = k.din("wg", [NF, 128, NDC * 128])
    wu_d = k.din("wu", [NF, 128, NDC * 128])
    wd_d = k.din("wd", [NDC, 128, NF * 128])
    win_d = k.din("win", [NZC, 128, NDC * 128])
    wing_d = k.din("wing", [128, NDC * 8])
    bin_d = k.din("binT", [128, NZC])
    bing_d = k.din("bing", [8, 1])
    if fz is None:
        zT_o = k.dout("zT", [NZC, 128, NTA], BF16)
        zg_o = k.dout("zg", [8, NTA])
        h1_o = k.dout("h1T", [NDC, 128, TPC])
    else:
        zg_o = fz["G1"]
        h1_o = fz["H1L"]

    cm = Common(k, W)
    ffn = FFN(k, cm, W)
    g1t = k.const_load("g1", g1, [128, NDC])
    g2t = k.const_load("g2", g2, [128, NDC])
    g3t = k.const_load("g3", g3, [128, NDC])
    bint = k.const_load("bint", bin_d, [128, NZC])
    bingt = k.const_load("bingt", bing_d, [8, 1])
    wing32 = k.const_load("wing32", wing_d, [128, NDC * 8])
    wing = k.sb("wingb", [128, NDC * 8], BF16)
    b_wing = Buf("wingb")
    P.add("dve", lambda e: e.tensor_copy(wing[:], wing32[0][:]), reads=[wing32[1]], writes=[b_wing])

    hT = k.sb("hT", [128, NDC, W], F32)
    b_h = Buf("hT")
    win_r = ffn.wg
    zo_r = Ring(k, "zo", [128, 512], BF16, 3)
    zg_t = k.sb("zgt", [8, W], F32)
    b_zg = Buf("zgt")
    psZ = [(k.psum("psZ"), Buf("psZ"))]
    psZ += [ffn.psG[0], ffn.psG[1], ffn.psU[0], ffn.psU[1]]

    SBS = [[(0, 0, 512), (512, TPC, NMETA)], [(0, 512, 512)]]
    xT_v = xT.rearrange("d p t -> p d t")
    h1_v = h1_o.rearrange("d p t -> p d t")
    zcnt = 0
    for sbi, segs3 in enumerate(SBS):
        segs = [(lo, n) for (lo, go, n) in segs3]
        for (lo, go, n) in segs3:
            P.dma("sp", hT[:, :, lo:lo + n], xT_v[:, :, go:go + n], writes=[b_h])
        ffn.run(hT, b_h, segs, g1t, g2t, wg_d, wu_d, wd_d)
        lo, go, n = segs3[0]
        if fz is None:
            k.out_dma(h1_v[:, :, go:go + n], hT[:, :, lo:lo + n], [b_h], "o_h1")
        else:
            bb = Buf("H1L%d" % sbi)
            fz["b_H1L"].append(bb)
            P.dma("sp", h1_v[:, :, go:go + n], hT[:, :, lo:lo + n], reads=[b_h], writes=[bb])
        ffn.prenorm(hT, b_h, g3t[0], g3t[1], segs)
        wl = {}

        def load_w(cc):
            t, b, kk = win_r.next()
            P.dma("pool", t[:], win_d[cc], writes=[b], sem=kk)
            wl[cc] = (t, b)

        for cc in range(3):
            load_w(cc)
        for cc in range(NZC):
            t, b = wl.pop(cc)
            for (lo, go, n) in segs3:
                if fz is not None and cc in LSLOT and go >= TPC:
                    continue
                ps, bps = psZ[zcnt % len(psZ)]
                zcnt += 1
                for dc in range(NDC):
                    P.add("pe", lambda e, dc=dc, ps=ps, t=t, lo=lo, n=n: e.matmul(
                        ps[:, :n], t[:, dc * 128:(dc + 1) * 128], ffn.uT[:, dc, lo:lo + n], start=(dc == 0), stop=(dc == NDC - 1)),
                        reads=[b, ffn.b_uT], writes=[bps])
                zo, bzo, kzo = zo_r.next()
                if zcnt % 2 == 0:
                    P.add("act", lambda e, zo=zo, ps=ps, cc=cc, n=n: e.activation(zo[:, :n], ps[:, :n], AF.Identity, bias=bint[0][:, cc:cc + 1]),
                          reads=[bps, bint[1]], writes=[bzo])
                else:
                    P.add("dve", lambda e, zo=zo, ps=ps, cc=cc, n=n: e.tensor_scalar(zo[:, :n], ps[:, :n], bint[0][:, cc:cc + 1], None, ALU.add),
                          reads=[bps, bint[1]], writes=[bzo])
                if fz is None:
                    k.out_dma(zT_o[cc, :, go:go + n], zo[:, :n], [bzo], kzo)
                elif cc in XSLOT:
                    bb = Buf("X1w")
                    fz["b_X1"].append(bb)
                    P.dma("sp", fz["X1"][XSLOT[cc], :, go:go + n], zo[:, :n], reads=[bzo], writes=[bb], sem=kzo)
                else:
                    bb = Buf("ZLw")
                    fz["b_ZL"].append(bb)
                    P.dma("sp", fz["ZL"][LSLOT[cc], :, go:go + n], zo[:, :n], reads=[bzo], writes=[bb], sem=kzo)
            if cc + 3 < NZC:
                load_w(cc + 3)
        for (lo, go, n) in segs3:
            ps, bps = psZ[zcnt % len(psZ)]
            zcnt += 1
            for dc in range(NDC):
                P.add("pe", lambda e, dc=dc, ps=ps, lo=lo, n=n: e.matmul(
                    ps[0:8, :n], wing[:, dc * 8:(dc + 1) * 8], ffn.uT[:, dc, lo:lo + n], start=(dc == 0), stop=(dc == NDC - 1)),
                    reads=[b_wing, ffn.b_uT], writes=[bps])
            P.add("act", lambda e, ps=ps, lo=lo, n=n: e.activation(zg_t[:, lo:lo + n], ps[0:8, :n], AF.Identity, bias=bingt[0][:, 0:1]),
                  reads=[bps, bingt[1]], writes=[b_zg])
            if fz is None:
                k.out_dma(zg_o[:, go:go + n], zg_t[:, lo:lo + n], [b_zg], "o_zg")
            else:
                bb = Buf("G1w")
                fz["b_G1"].append(bb)
                P.dma("sp", zg_o[:, go:go + n], zg_t[:, lo:lo + n], reads=[b_zg], writes=[bb])
    return k.finish()


def build_C(k=None, fz=None):
    if k is None:
        k = K()
    P = k.P
    W = 512
    if fz is None:
        hT_d = k.din("h_in", [NDC, 128, TPC])
        hm_d = k.din("hm", [8, 128, TPC])
        ha_d = k.din("ha", [8, 128, TPC])
        mo_d = k.din("mo", [8, 128, TPC], BF16)
        gm_d = k.din("gm", [NDC, 128, TPC], BF16)
        ga_d = k.din("ga", [NDC, 128, TPC], BF16)
        r_h, r_x2, r_zl = [], [], []
    else:
        hT_d = fz["H1L"]
        mo_d = fz["ZL"][0:8]
        gm_d = fz["ZL"][8:24]
        ga_d = fz["ZL"][24:40]
        r_h, r_zl = fz["b_H1L"], fz["b_ZL"]
        X2L = k.dint("X2L", [NCORES * 256, TPC])
        b_X2L = Buf("X2L")
        r_x2 = [b_X2L]
        nq = NCORES * 256 // 4
        for q_ in range(4):
            P.dma_fn(DYNQ, lambda e, q_=q_: e.dma_start(out=X2L[q_ * nq:(q_ + 1) * nq, :],
                                                         in_=fz["X2g"][q_ * nq:(q_ + 1) * nq, bass.ds(pid_of(e) * TPC, TPC)]),
                     reads=[fz["b_X2g"]], writes=[b_X2L])
        X2g3 = X2L.rearrange("(c h p) t -> c h p t", c=NCORES, h=2, p=128)

    def ld_x2(t, which, pc, go, n, b, kk):
        if fz is None:
            src = (hm_d if which == 0 else ha_d)[pc, :, go:go + n]
            P.dma("sp", t[:], src, writes=[b], sem=kk)
        else:
            P.dma("sp", t[:], X2g3[pc, which, :, go:go + n], reads=r_x2, writes=[b], sem=kk)
    gmn_d = k.din("g_mn", [128, 8])
    gan_d = k.din("g_an", [128, 8])
    gpost_d = k.din("g_mixpost", [128, NDC])
    g1 = k.din("g_pre2", [128, NDC])
    g2 = k.din("g_post2", [128, NDC])
    mwb_d = k.din("mwb", [NDC, 128, 8 * 128])
    awb_d = k.din("awb", [NDC, 128, 8 * 128])
    wout_d = k.din("wout", [NDC, 128, NDC * 128])
    wg_d = k.din("wg", [NF, 128, NDC * 128])
    wu_d = k.din("wu", [NF, 128, NDC * 128])
    wd_d = k.din("wd", [NDC, 128, NF * 128])
    out_o = k.dout("outT", [NDC, 128, TPC])

    cm = Common(k, W)
    ffn = FFN(k, cm, W)
    gmn = k.const_load("gmn", gmn_d, [128, 8])
    gan32 = k.const_load("gan32", gan_d, [128, 8])
    gan = (k.sb("gan", [128, 8], F32), Buf("gan"))
    P.add("dve", lambda e: e.tensor_scalar(gan[0][:], gan32[0][:], 1.0 - LAM_INIT, None, ALU.mult), reads=[gan32[1]], writes=[gan[1]])
    gpost = k.const_load("gpost", gpost_d, [128, NDC])
    g1t = k.const_load("g1", g1, [128, NDC])
    g2t = k.const_load("g2", g2, [128, NDC])

    hT = k.sb("hT", [128, NDC, W], F32)
    b_h = Buf("hT")
    st32 = Ring(k, "st32", [128, 512], F32, 4)
    st16 = Ring(k, "st16", [128, 512], BF16, 4)
    sig = Ring(k, "sig", [128, 512], F32, 3)
    tmp = Ring(k, "tmp", [128, 512], F32, 2)
    bw = ffn.wu
    hid = ffn.hid
    b_hid = ffn.b_hid
    psM, psA = ffn.psG, ffn.psU
    psO = ffn.psY
    hT_v = hT_d.rearrange("d p t -> p d t")
    out_v = out_o.rearrange("d p t -> p d t")
    cnt = 0
    for sbi in range(2):
        go = sbi * 512
        n = 512
        segs = [(0, 512)]
        P.dma("sp", hT[:, :, :], hT_v[:, :, go:go + n], reads=r_h, writes=[b_h])
        for hd in range(4):
            tiles = []
            for pc in (2 * hd, 2 * hd + 1):
                t, b, kk = st32.next()
                ld_x2(t, 0, pc, go, n, b, kk)
                tiles.append((t, b, pc))
            cm.rstd([(t[:], [b]) for (t, b, pc) in tiles], n, 256, ffn.rs, ffn.b_rs)
            for (t, b, pc) in tiles:
                g16, bg16, kg = st16.next()
                P.dma("sp", g16[:], mo_d[pc, :, go:go + n], reads=r_zl, writes=[bg16], sem=kg)
                sg, bsg, _ = sig.next()
                P.add("act", lambda e, sg=sg, g16=g16: e.activation(sg[:], g16[:], AF.Sigmoid), reads=[bg16], writes=[bsg])
                tm, btm, _ = tmp.next()
                P.add("dve", lambda e, tm=tm, t=t, pc=pc: e.scalar_tensor_tensor(tm[:], t[:], gmn[0][:, pc:pc + 1], ffn.rs[:, :n], ALU.mult, ALU.mult),
                      reads=[b, gmn[1], ffn.b_rs], writes=[btm])
                P.add("dve", lambda e, tm=tm, sg=sg, pc=pc: e.tensor_tensor(hid[:, pc, :], tm[:], sg[:], ALU.mult),
                      reads=[btm, bsg], writes=[b_hid[pc]])
        for hd in range(8):
            t, b, kk = st32.next()
            ld_x2(t, 1, hd, go, n, b, kk)
            cm.rstd([(t[:], [b])], n, 128, ffn.rs, ffn.b_rs)
            P.add("dve", lambda e, t=t, hd=hd: e.scalar_tensor_tensor(hid[:, 8 + hd, :], t[:], gan[0][:, hd:hd + 1], ffn.rs[:, :n], ALU.mult, ALU.mult),
                  reads=[b, gan[1], ffn.b_rs], writes=[b_hid[8 + hd]])
        for dc in range(NDC):
            wm, bwm, kwm = bw.next()
            wa, bwa, kwa = bw.next()
            P.dma("pool", wm[:, 0:1024], mwb_d[dc], writes=[bwm], sem=kwm)
            P.dma("pool", wa[:, 0:1024], awb_d[dc], writes=[bwa], sem=kwa)
            pm, bpm = psM[cnt % 2]
            pa, bpa = psA[cnt % 2]
            cnt += 1
            for pc in range(8):
                P.add("pe", lambda e, pc=pc, pm=pm, wm=wm: e.matmul(pm[:, :n], wm[:, pc * 128:(pc + 1) * 128], hid[:, pc, :], start=(pc == 0), stop=(pc == 7)),
                      reads=[bwm, b_hid[pc]], writes=[bpm])
            for pc in range(8):
                P.add("pe", lambda e, pc=pc, pa=pa, wa=wa: e.matmul(pa[:, :n], wa[:, pc * 128:(pc + 1) * 128], hid[:, 8 + pc, :], start=(pc == 0), stop=(pc == 7)),
                      reads=[bwa, b_hid[8 + pc]], writes=[bpa])
            g16a, bga, kga = st16.next()
            P.dma("sp", g16a[:], gm_d[dc, :, go:go + n], reads=r_zl, writes=[bga], sem=kga)
            g16b, bgb, kgb = st16.next()
            P.dma("sp", g16b[:], ga_d[dc, :, go:go + n], reads=r_zl, writes=[bgb], sem=kgb)
            s1, bs1, _ = sig.next()
            P.add("act", lambda e, s1=s1, g16a=g16a: e.activation(s1[:], g16a[:], AF.Sigmoid), reads=[bga], writes=[bs1])
            s2, bs2, _ = sig.next()
            P.add("act", lambda e, s2=s2, g16b=g16b: e.activation(s2[:], g16b[:], AF.Sigmoid), reads=[bgb], writes=[bs2])
            t1, bt1, _ = tmp.next()
            P.add("dve", lambda e, t1=t1, s1=s1, pm=pm: e.tensor_tensor(t1[:], s1[:], pm[:, :n], ALU.mult), reads=[bs1, bpm], writes=[bt1])
            t2, bt2, _ = tmp.next()
            P.add("dve", lambda e, t2=t2, s2=s2, pa=pa: e.tensor_tensor(t2[:], s2[:], pa[:, :n], ALU.mult), reads=[bs2, bpa], writes=[bt2])
            P.add("dve", lambda e, t1=t1, t2=t2, dc=dc: e.tensor_tensor(hid[:, 16 + dc, :], t1[:], t2[:], ALU.add),
                  reads=[bt1, bt2], writes=[b_hid[16 + dc]])
        for dcp in range(NDC):
            wo, bwo, kwo = ffn.wg.next()
            P.dma("pool", wo[:], wout_d[dcp], writes=[bwo], sem=kwo)
            po, bpo = psO[cnt % 2]
            cnt += 1
            for dc in range(NDC):
                P.add("pe", lambda e, dc=dc, po=po, wo=wo: e.matmul(po[:, :n], wo[:, dc * 128:(dc + 1) * 128], hid[:, 16 + dc, :], start=(dc == 0), stop=(dc == NDC - 1)),
                      reads=[bwo, b_hid[16 + dc]], writes=[bpo])
            P.add("act", lambda e, po=po, dcp=dcp: e.activation(ffn.yT[:, dcp, :], po[:, :n], AF.Copy), reads=[bpo], writes=[ffn.b_yT[dcp]])
        ffn.postnorm_residual(hT, b_h, ffn.yT, ffn.b_yT, segs, gpost, 1.0)
        ffn.run(hT, b_h, segs, g1t, g2t, wg_d, wu_d, wd_d)
        k.out_dma(out_v[:, :, go:go + n], hT[:, :, :], [b_h], "o_out")
    return k.finish()


def _gcol(g):
    return np.ascontiguousarray(g.reshape(-1, 128).T)


def _w_kmajor(w, ncol_chunk=128):
    Kd, N = w.shape
    kc, ncn = Kd // 128, N // ncol_chunk
    return np.ascontiguousarray(w.reshape(kc, 128, ncn, ncol_chunk).transpose(2, 1, 0, 3).reshape(ncn, 128, kc * ncol_chunk))


def _zcols():
    cols = []
    for (s, e) in ((0, 3072), (3080, 10248)):
        cols += list(range(s, e, 128))
    return cols


ZC = _zcols()
ZI = {"mq": 0, "mk": 4, "mv": 8, "mo": 16, "aq": 24, "ak": 32, "av": 40, "gm": 48, "ga": 64}


def prep_A(inp):
    x = inp["x"][0]
    meta = inp["meta"]
    shared = {
        "g_pre1": _gcol(inp["ffn1_pre_g"][0]), "g_post1": _gcol(inp["ffn1_post_g"][0]), "g_mix": _gcol(inp["mix_pre_g"][0]),
        "wg": _w_kmajor(inp["ffn1_w_gate"][0]), "wu": _w_kmajor(inp["ffn1_w_up"][0]), "wd": _w_kmajor(inp["ffn1_w_down"][0]),
    }
    w_in = inp["w_in"][0]
    b_in = inp["b_in"][0]
    wsel = np.concatenate([w_in[:, c:c + 128] for c in ZC], axis=1)
    shared["win"] = _w_kmajor(wsel)
    shared["binT"] = np.ascontiguousarray(np.stack([b_in[c:c + 128] for c in ZC], axis=1))
    gord = [3072 + 4 * (j % 2) + j // 2 for j in range(8)]
    shared["wing"] = _w_kmajor(np.ascontiguousarray(w_in[:, gord]), 8)[0]
    shared["bing"] = np.ascontiguousarray(b_in[gord].reshape(8, 1))
    maps = []
    for c in range(NCORES):
        xc = np.concatenate([x[c * TPC:(c + 1) * TPC], meta], axis=0)
        m = dict(shared)
        m["xT"] = np.ascontiguousarray(xc.T.reshape(NDC, 128, NTA))
        maps.append(m)
    return maps


PCS = 1024
MSCALE = 128 ** -0.5


def build_B(stage=9, k=None, fz=None):
    if k is None:
        k = K()
    P = k.P
    nc = k.nc
    cw_d = k.din("convw", [128, 8])
    cb_d = k.din("convb", [128, 2])
    cos_d = k.din("cosT", [128, LK])
    sin_d = k.din("sinT", [128, LK])
    lam_d = k.din("lam4", [128, 4 * 64])
    bi_ = {n_: Buf("in_" + n_) for n_ in ("mq", "mk", "mv", "gi", "gf", "aq", "aqs", "ak", "aks", "av")}
    if fz is None:
        mq_d = k.din("mq", [128, LK], BF16)
        mk_d = k.din("mk", [128, LK], BF16)
        mv_d = k.din("mv", [128, 65 * 128], BF16)
        gi_d = k.din("gi", [1, LK])
        gf_d = k.din("gf", [1, LK])
        aq_d = k.din("aq", [128, SEQ], BF16)
        aqs_d = k.din("aqs", [128, SEQ], BF16)
        ak_d = k.din("ak", [128, LK], BF16)
        aks_d = k.din("aks", [128, LK], BF16)
        av_d = k.din("av", [128, 65 * 128], BF16)
        hm_o = k.dout("hmT", [128, SEQ])
        ha_o = k.dout("haT", [128, SEQ])
    else:
        mq_d = k.dint("XB_mq", [128, LK], BF16)
        mk_d = k.dint("XB_mk", [128, LK], BF16)
        mvT_d = k.dint("XB_mvT", [128, LK], BF16)
        gi_d = k.dint("XB_gi", [1, LK])
        gf_d = k.dint("XB_gf", [1, LK])
        akf_d = k.dint("XB_ak", [128, LK], BF16)
        aksf_d = k.dint("XB_aks", [128, LK], BF16)
        avT_d = k.dint("XB_avT", [128, LK], BF16)
        ak_d, aks_d = akf_d, aksf_d
        aqf_d = k.dint("XB_aq", [128, LK], BF16)
        aqsf_d = k.dint("XB_aqs", [128, LK], BF16)
        aq_d, aqs_d = aqf_d[:, NMETA:LK], aqsf_d[:, NMETA:LK]
        hm_o = fz["X2"][0:128, :]
        ha_o = fz["X2"][128:256, :]
        X1g4 = fz["X1g"].rearrange("(r s p) t -> r s p t", r=NCORES, s=NXS, p=128)
        G1g3 = fz["G1g"].rearrange("(r g) t -> r g t", r=NCORES, g=8)
        TB1 = k.dint("TB1", [NCORES, 2, 128, NTA], BF16)
        TB2 = k.dint("TB2", [NCORES, 4, 128, NTA], BF16)
        TG = k.dint("TG", [NCORES, 2, NTA], F32)
        b_TB1, b_TB2, b_TG = Buf("TB1"), Buf("TB2"), Buf("TG")
        hr = NCORES // 2
        for r0 in (0, hr):
            P.dma_fn(DYNQ, lambda e, r0=r0: e.dma_start(
                out=TB1[r0:r0 + hr].rearrange("r s p t -> r (s p t)"),
                in_=X1g4[r0:r0 + hr, bass.ds(pid_of(e) // 2 * 2, 2), :, :].rearrange("r s p t -> r (s p t)")),
                reads=[fz["b_X1g"]], writes=[b_TB1])
            P.dma_fn(DYNQ, lambda e, r0=r0: e.dma_start(
                out=TB2[r0:r0 + hr].rearrange("r s p t -> r (s p t)"),
                in_=X1g4[r0:r0 + hr, bass.ds(pid_of(e) * 4 + 8, 4), :, :].rearrange("r s p t -> r (s p t)")),
                reads=[fz["b_X1g"]], writes=[b_TB2])
        P.dma_fn(DYNQ, lambda e: e.dma_start(out=TG[:].rearrange("r g t -> r (g t)"),
                                             in_=G1g3[:, bass.ds(pid_of(e) // 2 * 2, 2), :].rearrange("r g t -> r (g t)")),
                 reads=[fz["b_G1g"]], writes=[b_TG])

        def unpack(dst, bdst, T, bT, si, rows=None):
            rows = rows or [(0, 0, 128)]
            for (d0, s0, nr) in rows:
                P.dma("sp", dst[d0:d0 + nr, NMETA:LK].rearrange("p (r t) -> p r t", r=NCORES),
                      T[:, si, s0:s0 + nr, 0:TPC].rearrange("r p t -> p r t"), reads=[bT], writes=[bdst])
                P.dma("sp", dst[d0:d0 + nr, 0:NMETA], T[0, si, s0:s0 + nr, TPC:NTA], reads=[bT], writes=[bdst])

        prow = []
        for m_ in range(2):
            prow += [(m_ * 64, m_ * 64 + 8, 8), (m_ * 64 + 8, m_ * 64, 8), (m_ * 64 + 16, m_ * 64 + 16, 48)]
        unpack(mq_d, bi_["mq"], TB1, b_TB1, 0)
        unpack(mk_d, bi_["mk"], TB1, b_TB1, 1)
        unpack(mvT_d, bi_["mv"], TB2, b_TB2, 0)
        unpack(aqf_d, bi_["aq"], TB2, b_TB2, 1)
        unpack(aqsf_d, bi_["aqs"], TB2, b_TB2, 1, prow)
        unpack(akf_d, bi_["ak"], TB2, b_TB2, 2)
        unpack(aksf_d, bi_["aks"], TB2, b_TB2, 2, prow)
        unpack(avT_d, bi_["av"], TB2, b_TB2, 3)
        for (dst, bdst, gi_) in ((gi_d, bi_["gi"], 0), (gf_d, bi_["gf"], 1)):
            P.dma("sp", dst[0:1, NMETA:LK].rearrange("o (r t) -> o r t", r=NCORES), TG[:, gi_:gi_ + 1, 0:TPC].rearrange("r o t -> o r t"),
                  reads=[b_TG], writes=[bdst])
            P.dma("sp", dst[0:1, 0:NMETA], TG[0, gi_:gi_ + 1, TPC:NTA], reads=[b_TG], writes=[bdst])
    dA = nc.dram_tensor("dA", [1, LK], F32).ap()
    dM = nc.dram_tensor("dM", [1, LK], F32).ap()
    dE = nc.dram_tensor("dE", [1, LK], F32).ap()

    f32r = Ring(k, "f32r", [128, PCS], F32, 6)
    b16r = Ring(k, "b16r", [128, PCS + 4], BF16, 4)
    ones = k.sb("ones_bf", [128, 128], BF16)
    b_ones = Buf("ones")
    P.add("pool", lambda e: e.memset(ones[:], 1.0), writes=[b_ones])
    tri = k.sb("tri", [128, 128], F32)
    b_tri = Buf("tri")
    P.add("pool", lambda e: e.memset(tri[:], 1.0), writes=[b_tri])
    P.add("pool", lambda e: e.affine_select(tri[:], tri[:], [[1, 128]], ALU.is_ge, 0.0, base=0, channel_multiplier=-1),
          reads=[b_tri], writes=[b_tri])
    zeros = k.sb("zeros", [1, PCS], F32)
    b_zeros = Buf("zeros")
    P.add("pool", lambda e: e.memset(zeros[:], 0.0), writes=[b_zeros])
    carB = k.sb("carB", [1, 1], F32)
    carM = k.sb("carM", [1, 1], F32)
    b_carB, b_carM = Buf("carB"), Buf("carM")
    P.add("pool", lambda e: e.memset(carB[:], 0.0), writes=[b_carB])
    P.add("pool", lambda e: e.memset(carM[:], 0.0), writes=[b_carM])
    cw = k.const_load("cw", cw_d, [128, 8])
    cb = k.const_load("cb", cb_d, [128, 2])

    b_dA, b_dM, b_dE = Buf("dA"), Buf("dM"), Buf("dE")
    pieces = [(0, NMETA)] + [(NMETA + i * PCS, PCS) for i in range(SEQ // PCS)]
    for (c0, n) in pieces:
        ti, bi, ki = f32r.next()
        tf, bf_, kf = f32r.next()
        s1, bs1, _ = f32r.next()
        s2, bs2, _ = f32r.next()
        s3, bs3, _ = f32r.next()
        P.dma("sp", ti[0:1, :n], gi_d[:, c0:c0 + n], reads=[bi_["gi"]], writes=[bi], sem=ki)
        P.dma("sp", tf[0:1, :n], gf_d[:, c0:c0 + n], reads=[bi_["gf"]], writes=[bf_], sem=kf)
        P.add("act", lambda e, s1=s1, tf=tf, n=n: e.activation(s1[0:1, :n], tf[0:1, :n], AF.Abs), reads=[bf_], writes=[bs1])
        P.add("act", lambda e, s1=s1, n=n: e.activation(s1[0:1, :n], s1[0:1, :n], AF.Exp, scale=-1.0), reads=[bs1], writes=[bs1])
        P.add("act", lambda e, s1=s1, n=n: e.activation(s1[0:1, :n], s1[0:1, :n], AF.Ln, bias=1.0), reads=[bs1], writes=[bs1])
        P.add("dve", lambda e, s2=s2, tf=tf, n=n: e.tensor_scalar_min(s2[0:1, :n], tf[0:1, :n], 0.0), reads=[bf_], writes=[bs2])
        P.add("dve", lambda e, s1=s1, s2=s2, n=n: e.tensor_tensor(s2[0:1, :n], s2[0:1, :n], s1[0:1, :n], ALU.subtract), reads=[bs1, bs2], writes=[bs2])
        P.add("dve", lambda e, s2=s2, s3=s3, n=n: e.tensor_tensor_scan(s3[0:1, :n], s2[0:1, :n], zeros[0:1, :n], carB[0:1, 0:1], ALU.add, ALU.add),
              reads=[bs2, b_zeros, b_carB], writes=[bs3])
        P.add("dve", lambda e, s3=s3, n=n: e.tensor_copy(carB[0:1, 0:1], s3[0:1, n - 1:n]), reads=[bs3], writes=[b_carB])
        P.add("dve", lambda e, s1=s1, ti=ti, s3=s3, n=n: e.tensor_tensor(s1[0:1, :n], ti[0:1, :n], s3[0:1, :n], ALU.subtract), reads=[bi, bs3], writes=[bs1])
        P.add("dve", lambda e, s1=s1, s2=s2, n=n: e.tensor_tensor_scan(s2[0:1, :n], s1[0:1, :n], s1[0:1, :n], carM[0:1, 0:1], ALU.max, ALU.max),
              reads=[bs1, b_carM], writes=[bs2])
        P.add("dve", lambda e, s2=s2, n=n: e.tensor_copy(carM[0:1, 0:1], s2[0:1, n - 1:n]), reads=[bs2], writes=[b_carM])
        P.add("dve", lambda e, s2=s2, s3=s3, n=n: e.tensor_tensor(s3[0:1, :n], s3[0:1, :n], s2[0:1, :n], ALU.add), reads=[bs2, bs3], writes=[bs3])
        P.add("act", lambda e, s3=s3, n=n: e.activation(s3[0:1, :n], s3[0:1, :n], AF.Exp, scale=-1.0), reads=[bs3], writes=[bs3])
        P.dma("sp", dA[:, c0:c0 + n], s1[0:1, :n], reads=[bs1], writes=[b_dA])
        P.dma("sp", dM[:, c0:c0 + n], s2[0:1, :n], reads=[bs2], writes=[b_dM])
        P.dma("sp", dE[:, c0:c0 + n], s3[0:1, :n], reads=[bs3], writes=[b_dE])
    AlT = k.sb("AlT", [128, 65], F32)
    b_AlT = Buf("AlT")
    P.add("pool", lambda e: e.memset(AlT[:], 0.0), writes=[b_AlT])
    ident = k.sb("ident", [128, 128], F32)
    b_ident = Buf("ident")
    P.add("pool", lambda e: e.memset(ident[:], 0.0), writes=[b_ident])
    P.add("pool", lambda e: e.affine_select(ident[:], ident[:], [[1, 128]], ALU.not_equal, 1.0, base=0, channel_multiplier=-1),
          reads=[b_ident], writes=[b_ident])
    identb = k.sb("identb", [128, 128], BF16)
    b_identb = Buf("identb")
    P.add("dve", lambda e: e.tensor_copy(identb[:], ident[:]), reads=[b_ident], writes=[b_identb])
    arow, barow, karow = f32r.next()
    P.dma("sp", arow[0:64, 0:128], dA[0, NMETA:LK].rearrange("(kk j) -> kk j", j=128), reads=[b_dA], writes=[barow], sem=karow)
    P.dma("sp", arow[0:1, 128:128 + NMETA], dA[0:1, 0:NMETA], reads=[b_dA], writes=[barow], sem=karow)
    psD = [(k.psum("psD%d" % i), Buf("psD%d" % i)) for i in range(2)]
    psT, b_psT = psD[0]
    P.add("pe", lambda e: e.transpose(psT[:, 0:64], arow[0:64, 0:128], ident[0:64, 0:64]), reads=[barow, b_ident], writes=[b_psT])
    P.add("pe", lambda e: e.transpose(psT[0:NMETA, 64:65], arow[0:1, 128:128 + NMETA], ident[0:1, 0:1]), reads=[barow, b_ident], writes=[b_psT])
    P.add("dve", lambda e: e.tensor_copy(AlT[:, 0:64], psT[:, 0:64]), reads=[b_psT], writes=[b_AlT])
    P.add("dve", lambda e: e.tensor_copy(AlT[0:NMETA, 64:65], psT[0:NMETA, 64:65]), reads=[b_psT], writes=[b_AlT])
    P.add("dve", lambda e: e.tensor_scalar(AlT[:], AlT[:], math.log(MSCALE), None, ALU.add), reads=[b_AlT], writes=[b_AlT])
    MuB = k.sb("MuB", [128, SEQ], F32)
    b_MuB = Buf("MuB")
    EB = k.sb("EB", [128, SEQ], F32)
    b_EB = Buf("EB")
    P.dma("sp", MuB[:], dM[0:1, NMETA:LK].partition_broadcast(128), reads=[b_dM], writes=[b_MuB])
    P.dma("sp", EB[:], dE[0:1, NMETA:LK].partition_broadcast(128), reads=[b_dE], writes=[b_EB])

    if stage <= 1:
        return k.finish()
    qT = k.sb("qT", [128, LK], BF16)
    kT = k.sb("kT", [128, LK], BF16)
    b_qT, b_kT = Buf("qT"), Buf("kT")
    cpieces = [(i * PCS, PCS) for i in range(LK // PCS)] + [((LK // PCS) * PCS, LK % PCS)]
    for wi, (src, dst, bdst, bsrc) in enumerate(((mq_d, qT, b_qT, bi_["mq"]), (mk_d, kT, b_kT, bi_["mk"]))):
        for (c0, n) in cpieces:
            raw, braw, kraw = b16r.next()
            if c0 == 0:
                P.add("pool", lambda e, raw=raw: e.memset(raw[:, 0:3], 0.0), writes=[braw])
                P.dma("sp", raw[:, 3:3 + n], src[:, 0:n], reads=[bsrc], writes=[braw], sem=kraw)
            else:
                P.dma("sp", raw[:, 0:3 + n], src[:, c0 - 3:c0 + n], reads=[bsrc], writes=[braw], sem=kraw)
            acc, bacc, _ = f32r.next()
            P.add("dve", lambda e, acc=acc, raw=raw, n=n, wi=wi: e.tensor_scalar(
                acc[:, :n], raw[:, 0:n], cw[0][:, 4 * wi:4 * wi + 1], cb[0][:, wi:wi + 1], ALU.mult, ALU.add),
                reads=[braw, cw[1], cb[1]], writes=[bacc])
            for t in range(1, 4):
                P.add("dve", lambda e, acc=acc, raw=raw, n=n, wi=wi, t=t: e.scalar_tensor_tensor(
                    acc[:, :n], raw[:, t:t + n], cw[0][:, 4 * wi + t:4 * wi + t + 1], acc[:, :n], ALU.mult, ALU.add),
                    reads=[braw, cw[1], bacc], writes=[bacc])
            P.add("act", lambda e, acc=acc, dst=dst, c0=c0, n=n: e.activation(dst[:, c0:c0 + n], acc[:, :n], AF.Silu), reads=[bacc], writes=[bdst])
    mv = k.sb("mv_sb", [128, 65 * 128], BF16)
    b_mv = Buf("mv")
    psS = [(k.psum("psS%d" % i, (128, 1024)), Buf("psS%d" % i)) for i in range(2)]

    def load_v(vT_d, bsrc):
        P.add("pool", lambda e: e.memset(mv[:, 64 * 128:65 * 128], 0.0), writes=[b_mv])
        vp = [(NMETA + i * PCS, PCS, i * 8) for i in range(SEQ // PCS)] + [(0, NMETA, 64)]
        for pi, (c0, n, kk0) in enumerate(vp):
            raw, braw, kraw = b16r.next()
            P.dma("sp", raw[:, 0:n], vT_d[:, c0:c0 + n], reads=[bsrc], writes=[braw], sem=kraw)
            ps, bps = psS[pi % 2]
            psb = ps[:].bitcast(BF16)
            nch = (n + 127) // 128
            for j in range(nch):
                w_ = min(128, n - j * 128)
                P.add("pe", lambda e, psb=psb, raw=raw, j=j, w_=w_: e.transpose(psb[0:w_, j * 128:(j + 1) * 128], raw[:, j * 128:j * 128 + w_], identb[:]),
                      reads=[braw, b_identb], writes=[bps])
            rows = min(128, n)
            P.add("act" if pi % 2 else "dve", (lambda e, psb=psb, kk0=kk0, nch=nch, rows=rows: e.activation(mv[0:rows, kk0 * 128:(kk0 + nch) * 128], psb[0:rows, 0:nch * 128], AF.Copy)) if pi % 2 else
                  (lambda e, psb=psb, kk0=kk0, nch=nch, rows=rows: e.tensor_copy(mv[0:rows, kk0 * 128:(kk0 + nch) * 128], psb[0:rows, 0:nch * 128])),
                  reads=[bps], writes=[b_mv])

    if fz is None:
        P.dma("sp", mv[:], mv_d, reads=[bi_["mv"]], writes=[b_mv])
    else:
        load_v(mvT_d, bi_["mv"])

    if stage <= 2:
        return k.finish()
    psN = [(k.psum("psN%d" % i), Buf("psN%d" % i)) for i in range(2)]
    Wr = Ring(k, "Wt", [128, 512], F32, 3)
    Pr = Ring(k, "Pt", [128, 512], BF16, 3)
    ost = Ring(k, "ost", [128, 512], F32, 2)
    rcp = Ring(k, "rcp", [128, 512], F32, 2)

    scnt = 0
    for qb in range(SEQ // 512):
        q0 = qb * 512
        pn, bpn = psN[qb % 2]
        pd, bpd = psD[qb % 2]
        klist = [64] + list(range(4 * qb + 4))
        for ki, kk in enumerate(klist):
            ksz = NMETA if kk == 64 else 128
            kc0 = 0 if kk == 64 else NMETA + kk * 128
            ps, bps = psS[scnt % 2]
            scnt += 1
            P.add("pe", lambda e, ps=ps, ksz=ksz, kc0=kc0, q0=q0: e.matmul(
                ps[:ksz, 0:512], kT[:, kc0:kc0 + ksz], qT[:, NMETA + q0:NMETA + q0 + 512], start=True, stop=True),
                reads=[b_kT, b_qT], writes=[bps])
            wt, bwt, _ = Wr.next()
            P.add("act", lambda e, wt=wt, ksz=ksz, kk=kk, q0=q0: e.activation(
                wt[:ksz, :], MuB[:ksz, q0:q0 + 512], AF.Exp, bias=AlT[:ksz, kk:kk + 1], scale=-1.0),
                reads=[b_MuB, b_AlT], writes=[bwt])
            r = kk - 4 * qb
            if kk != 64 and r >= 0:
                if r > 0:
                    P.add("pool", lambda e, wt=wt, r=r: e.memset(wt[:, 0:128 * r], 0.0), writes=[bwt])
                P.add("pool", lambda e, wt=wt, r=r: e.tensor_tensor(wt[:, 128 * r:128 * r + 128], wt[:, 128 * r:128 * r + 128], tri[:], ALU.mult),
                      reads=[b_tri, bwt], writes=[bwt])
            pt, bpt, _ = Pr.next()
            P.add("dve", lambda e, pt=pt, wt=wt, ps=ps, ksz=ksz: e.tensor_tensor(pt[:ksz, :], wt[:ksz, :], ps[:ksz, 0:512], ALU.mult),
                  reads=[bwt, bps], writes=[bpt])
            first, last = (ki == 0), (ki == len(klist) - 1)
            voff = kk * 128
            P.add("pe", lambda e, pn=pn, pt=pt, ksz=ksz, voff=voff, first=first, last=last: e.matmul(
                pn[:, :], mv[:ksz, voff:voff + 128], pt[:ksz, :], start=first, stop=last), reads=[b_mv, bpt], writes=[bpn])
            P.add("pe", lambda e, pd=pd, pt=pt, ksz=ksz, first=first, last=last: e.matmul(
                pd[:, :], ones[:ksz, :], pt[:ksz, :], start=first, stop=last), reads=[b_ones, bpt], writes=[bpd])
        rc, brc, _ = rcp.next()
        P.add("act", lambda e, rc=rc, pd=pd: e.activation(rc[:], pd[:, :], AF.Abs), reads=[bpd], writes=[brc])
        P.add("dve", lambda e, rc=rc, q0=q0: e.tensor_tensor(rc[:], rc[:], EB[:, q0:q0 + 512], ALU.max), reads=[brc, b_EB], writes=[brc])
        P.add("dve", lambda e, rc=rc: e.reciprocal(rc[:], rc[:]), reads=[brc], writes=[brc])
        o, bo, ko = ost.next()
        P.add("dve", lambda e, o=o, rc=rc, pn=pn: e.tensor_tensor(o[:], rc[:], pn[:, :], ALU.mult), reads=[brc, bpn], writes=[bo])
        if fz is None:
            k.out_dma(hm_o[:, q0:q0 + 512], o[:], [bo], ko)
        else:
            bb = Buf("X2w")
            fz["b_X2"].append(bb)
            P.dma("sp", hm_o[:, q0:q0 + 512], o[:], reads=[bo], writes=[bb], sem=ko)

    if stage <= 3:
        return k.finish()
    lam4 = k.const_load("lam4_sb", lam_d, [128, 256])
    lt = k.sb("lamt", [128, 8], F32)
    b_lt = Buf("lamt")
    lsc = k.sb("lamsc", [128, 128], F32)
    b_lsc = Buf("lamsc")
    P.add("dve", lambda e: e.tensor_tensor(lsc[:, 0:64], lam4[0][:, 0:64], lam4[0][:, 64:128], ALU.mult), reads=[lam4[1]], writes=[b_lsc])
    P.add("dve", lambda e: e.tensor_tensor(lsc[:, 64:128], lam4[0][:, 128:192], lam4[0][:, 192:256], ALU.mult), reads=[lam4[1], b_lsc], writes=[b_lsc])
    P.add("dve", lambda e: e.tensor_reduce(lt[:, 0:2], lsc[:].rearrange("p (a b) -> p a b", a=2), AX.X, ALU.add), reads=[b_lsc], writes=[b_lt])
    P.add("act", lambda e: e.activation(lt[:, 2:4], lt[:, 0:2], AF.Exp), reads=[b_lt], writes=[b_lt])
    P.add("dve", lambda e: e.tensor_tensor(lt[:, 4:5], lt[:, 3:4], lt[:, 2:3], ALU.subtract), reads=[b_lt], writes=[b_lt])
    P.add("dve", lambda e: e.tensor_scalar(lt[:, 5:6], lt[:, 4:5], -LAM_INIT, None, ALU.add), reads=[b_lt], writes=[b_lt])

    aqT, akT = kT, qT
    b_aqT, b_akT = b_kT, b_qT
    for (xd, xsd, dst, bdst, L_, poff, bx_in, bxs_in) in ((ak_d, aks_d, akT, b_akT, LK, 0, bi_["ak"], bi_["aks"]),
                                                           (aq_d, aqs_d, aqT, b_aqT, SEQ, NMETA, bi_["aq"], bi_["aqs"])):
        pcs = [(i * PCS, PCS) for i in range(L_ // PCS)]
        if L_ % PCS:
            pcs.append(((L_ // PCS) * PCS, L_ % PCS))
        for (c0, n) in pcs:
            x, bx, kx = b16r.next()
            xs, bxs, kxs = b16r.next()
            ct, bct, kct = f32r.next()
            stt_, bst, kst = f32r.next()
            P.dma("sp", x[:, :n], xd[:, c0:c0 + n], reads=[bx_in], writes=[bx], sem=kx)
            P.dma("sp", xs[:, :n], xsd[:, c0:c0 + n], reads=[bxs_in], writes=[bxs], sem=kxs)
            P.dma("sp", ct[:, :n], cos_d[:, poff + c0:poff + c0 + n], writes=[bct], sem=kct)
            P.dma("sp", stt_[:, :n], sin_d[:, poff + c0:poff + c0 + n], writes=[bst], sem=kst)
            P.add("dve", lambda e, ct=ct, x=x, n=n: e.tensor_tensor(ct[:, :n], ct[:, :n], x[:, :n], ALU.mult), reads=[bx, bct], writes=[bct])
            P.add("pool", lambda e, stt_=stt_, xs=xs, n=n: e.tensor_tensor(stt_[:, :n], stt_[:, :n], xs[:, :n], ALU.mult), reads=[bxs, bst], writes=[bst])
            P.add("dve", lambda e, dst=dst, ct=ct, stt_=stt_, c0=c0, n=n: e.tensor_tensor(dst[:, c0:c0 + n], ct[:, :n], stt_[:, :n], ALU.add),
                  reads=[bct, bst], writes=[bdst])
    av, b_av = mv, b_mv
    if fz is None:
        P.dma("sp", av[:], av_d, reads=[bi_["av"]], writes=[b_av])
    else:
        load_v(avT_d, bi_["av"])

    if stage <= 4:
        return k.finish()
    for qb in range(SEQ // 256):
        q0 = qb * 256
        pn, bpn = psN[qb % 2]
        pd, bpd = psD[qb % 2]
        klist = [64] + list(range(2 * qb + 2))
        for ki, kk in enumerate(klist):
            ksz = NMETA if kk == 64 else 128
            kc0 = 0 if kk == 64 else NMETA + kk * 128
            ps, bps = psS[scnt % 2]
            scnt += 1
            for m in range(2):
                P.add("pe", lambda e, ps=ps, ksz=ksz, kc0=kc0, q0=q0, m=m: e.matmul(
                    ps[:ksz, m * 512:m * 512 + 256], akT[m * 64:(m + 1) * 64, kc0:kc0 + ksz], aqT[m * 64:(m + 1) * 64, q0:q0 + 256],
                    start=True, stop=True), reads=[b_akT, b_aqT], writes=[bps])
            pt, bpt, _ = Pr.next()
            P.add("act", lambda e, pt=pt, ps=ps, ksz=ksz: e.activation(
                pt[:ksz, :].rearrange("p (m c) -> p m c", m=2), ps[:ksz, :].rearrange("p (m c) -> p m c", m=2)[:, :, 0:256], AF.Exp, scale=0.125),
                reads=[bps], writes=[bpt])
            r = kk - 2 * qb
            if kk != 64 and r >= 0:
                ptv = pt[:].rearrange("p (m c) -> p m c", m=2)
                if r == 0:
                    P.add("pool", lambda e, ptv=ptv: e.memset(ptv[64:128, :, 0:64], 0.0), writes=[bpt])
                else:
                    P.add("pool", lambda e, ptv=ptv: e.memset(ptv[:, :, 0:128], 0.0), writes=[bpt])
                    P.add("pool", lambda e, ptv=ptv: e.memset(ptv[64:128, :, 128:192], 0.0), writes=[bpt])
            first, last = (ki == 0), (ki == len(klist) - 1)
            voff = kk * 128
            P.add("pe", lambda e, pn=pn, pt=pt, ksz=ksz, voff=voff, first=first, last=last: e.matmul(
                pn[:, :], av[:ksz, voff:voff + 128], pt[:ksz, :], start=first, stop=last), reads=[b_av, bpt], writes=[bpn])
            P.add("pe", lambda e, pd=pd, pt=pt, ksz=ksz, first=first, last=last: e.matmul(
                pd[:, :], ones[:ksz, :], pt[:ksz, :], start=first, stop=last), reads=[b_ones, bpt], writes=[bpd])
        rc, brc, _ = rcp.next()
        P.add("dve", lambda e, rc=rc, pd=pd: e.reciprocal(rc[:], pd[:, :]), reads=[bpd], writes=[brc])
        P.add("dve", lambda e, rc=rc, pn=pn: e.tensor_tensor(rc[:], rc[:], pn[:, :], ALU.mult), reads=[brc, bpn], writes=[brc])
        o, bo, ko = ost.next()
        P.add("dve", lambda e, o=o, rc=rc: e.scalar_tensor_tensor(o[:, 0:256], rc[:, 256:512], lt[:, 5:6], rc[:, 0:256], ALU.mult, ALU.add),
              reads=[brc, b_lt], writes=[bo])
        if fz is None:
            k.out_dma(ha_o[:, q0:q0 + 256], o[:, 0:256], [bo], ko)
        else:
            bb = Buf("X2w")
            fz["b_X2"].append(bb)
            P.dma("sp", ha_o[:, q0:q0 + 256], o[:, 0:256], reads=[bo], writes=[bb], sem=ko)
    return k.finish()


def _rope_tables():
    half = 8
    inv = ROPE_THETA ** (-np.arange(0, 16, 2, dtype=np.float32) / 16.0)
    ang = np.arange(LK, dtype=np.float32)[:, None] * inv[None, :]
    cos, sin = np.cos(ang).T.astype(np.float32), np.sin(ang).T.astype(np.float32)
    cosT = np.ones((128, LK), np.float32)
    sinT = np.zeros((128, LK), np.float32)
    for m in range(2):
        cosT[m * 64:m * 64 + 8] = cos
        cosT[m * 64 + 8:m * 64 + 16] = cos
        sinT[m * 64:m * 64 + 8] = -sin
        sinT[m * 64 + 8:m * 64 + 16] = sin
    return cosT, sinT


def _partner_rows():
    idx = np.arange(128)
    r = idx % 64
    return np.where(r < 8, idx + 8, np.where(r < 16, idx - 8, idx))


def _tokmajor(vT):
    out = np.zeros((128, 65, 128), vT.dtype)
    out[:, :64, :] = vT[:, NMETA:].T.reshape(64, 128, 128).transpose(1, 0, 2)
    out[:NMETA, 64, :] = vT[:, :NMETA].T
    return np.ascontiguousarray(out.reshape(128, 65 * 128))


def prep_B(inp, resA):
    zT = [np.asarray(r["zT"]) for r in resA]
    zg = [np.asarray(r["zg"]) for r in resA]

    def zfull(ci):
        return np.concatenate([zT[0][ci][:, TPC:NTA]] + [zT[c][ci][:, :TPC] for c in range(NCORES)], axis=1)

    def gfull(row):
        return np.ascontiguousarray(np.concatenate([zg[0][row:row + 1, TPC:NTA]] + [zg[c][row:row + 1, :TPC] for c in range(NCORES)], axis=1))

    cosT, sinT = _rope_tables()
    prow = _partner_rows()
    cwf = inp["m_conv_w"][0]
    cbf = inp["m_conv_b"][0]
    lam4 = np.ascontiguousarray(np.broadcast_to(np.concatenate(
        [inp["a_lambda_q1"][0], inp["a_lambda_k1"][0], inp["a_lambda_q2"][0], inp["a_lambda_k2"][0]])[None, :], (128, 256)))
    maps = []
    for c in range(NCORES):
        hd, half = c // 2, c % 2
        aq = zfull(ZI["aq"] + c)
        ak = zfull(ZI["ak"] + c)
        m = {
            "mq": zfull(ZI["mq"] + hd), "mk": zfull(ZI["mk"] + hd),
            "convw": np.ascontiguousarray(np.concatenate([cwf[:, hd * 128:(hd + 1) * 128].T, cwf[:, 512 + hd * 128:512 + (hd + 1) * 128].T], axis=1)),
            "convb": np.ascontiguousarray(np.stack([cbf[hd * 128:(hd + 1) * 128], cbf[512 + hd * 128:512 + (hd + 1) * 128]], axis=1)),
            "mv": _tokmajor(zfull(ZI["mv"] + 2 * hd + half)),
            "gi": gfull(2 * hd), "gf": gfull(2 * hd + 1),
            "aq": np.ascontiguousarray(aq[:, NMETA:]), "aqs": np.ascontiguousarray(aq[prow][:, NMETA:]),
            "ak": ak, "aks": np.ascontiguousarray(ak[prow]),
            "av": _tokmajor(zfull(ZI["av"] + c)),
            "cosT": cosT, "sinT": sinT, "lam4": lam4,
        }
        maps.append(m)
    return maps


def prep_C(inp, resA, resB):
    shared = {
        "g_mn": _gcol(inp["m_norm_g"][0]), "g_an": _gcol(inp["a_norm_g"][0]), "g_mixpost": _gcol(inp["mix_post_g"][0]),
        "g_pre2": _gcol(inp["ffn2_pre_g"][0]), "g_post2": _gcol(inp["ffn2_post_g"][0]),
        "mwb": _w_kmajor(inp["m_w_branch"][0]), "awb": _w_kmajor(inp["a_w_branch"][0]), "wout": _w_kmajor(inp["w_out"][0]),
        "wg": _w_kmajor(inp["ffn2_w_gate"][0]), "wu": _w_kmajor(inp["ffn2_w_up"][0]), "wd": _w_kmajor(inp["ffn2_w_down"][0]),
    }
    hm = [np.asarray(r["hmT"]) for r in resB]
    ha = [np.asarray(r["haT"]) for r in resB]
    maps = []
    for r in range(NCORES):
        zT = np.asarray(resA[r]["zT"])
        sl = slice(r * TPC, (r + 1) * TPC)
        m = dict(shared)
        m["h_in"] = np.asarray(resA[r]["h1T"])
        m["hm"] = np.ascontiguousarray(np.stack([hm[c][:, sl] for c in range(NCORES)]))
        m["ha"] = np.ascontiguousarray(np.stack([ha[c][:, sl] for c in range(NCORES)]))
        m["mo"] = np.ascontiguousarray(zT[ZI["mo"]:ZI["mo"] + 8, :, :TPC])
        m["gm"] = np.ascontiguousarray(zT[ZI["gm"]:ZI["gm"] + 16, :, :TPC])
        m["ga"] = np.ascontiguousarray(zT[ZI["ga"]:ZI["ga"] + 16, :, :TPC])
        maps.append(m)
    return maps


_CACHE = {}


def _prog(name):
    if name not in _CACHE:
        _CACHE[name] = {"A": build_A, "B": build_B, "C": build_C}[name]()
    return _CACHE[name]


def _allgather(k, src, dst, reads, bdst):
    key = k.P.newkey("cc")
    k.P.dma_fn("pool", lambda e: e.collective_compute("AllGather", ALU.bypass, replica_groups=[list(range(NCORES))],
                                                      ins=[src.opt()], outs=[dst.opt()]),
               reads=reads, writes=[bdst], sem=key, inc=1)


def build_fused():
    _PID.clear()
    k = K()
    k.fused = True
    k.P.dyn_prologue = dyn_prologue
    fz = {
        "X1": k.dint("X1", [NXS * 128, NTA], BF16).rearrange("(s p) t -> s p t", p=128),
        "X1g": k.dint("X1g", [NCORES * NXS * 128, NTA], BF16),
        "G1": k.dint("G1", [8, NTA]),
        "G1g": k.dint("G1g", [NCORES * 8, NTA]),
        "ZL": k.dint("ZL", [NXS * 128, TPC], BF16).rearrange("(s p) t -> s p t", p=128),
        "H1L": k.dint("H1L", [NDC * 128, TPC]).rearrange("(d p) t -> d p t", p=128),
        "X2": k.dint("X2", [256, SEQ]),
        "X2g": k.dint("X2g", [NCORES * 256, SEQ]),
        "b_X1": [], "b_G1": [], "b_ZL": [], "b_H1L": [], "b_X2": [],
        "b_X1g": Buf("X1g"), "b_G1g": Buf("G1g"), "b_X2g": Buf("X2g"),
    }
    k.begin_phase("A_")
    build_A(k, fz)
    k.end_phase()
    _allgather(k, fz["X1"].rearrange("s p t -> (s p) t"), fz["X1g"], fz["b_X1"], fz["b_X1g"])
    _allgather(k, fz["G1"], fz["G1g"], fz["b_G1"], fz["b_G1g"])
    k.begin_phase("B_")
    build_B(9, k, fz)
    k.end_phase()
    _allgather(k, fz["X2"], fz["X2g"], fz["b_X2"], fz["b_X2g"])
    k.begin_phase("C_")
    build_C(k, fz)
    k.st.close()
    return k.finish_fused()


def prep_fused(inp):
    mA = prep_A(inp)
    cosT, sinT = _rope_tables()
    cwf = inp["m_conv_w"][0]
    cbf = inp["m_conv_b"][0]
    lam4 = np.ascontiguousarray(np.broadcast_to(np.concatenate(
        [inp["a_lambda_q1"][0], inp["a_lambda_k1"][0], inp["a_lambda_q2"][0], inp["a_lambda_k2"][0]])[None, :], (128, 256)))
    sharedC = {
        "g_mn": _gcol(inp["m_norm_g"][0]), "g_an": _gcol(inp["a_norm_g"][0]), "g_mixpost": _gcol(inp["mix_post_g"][0]),
        "g_pre2": _gcol(inp["ffn2_pre_g"][0]), "g_post2": _gcol(inp["ffn2_post_g"][0]),
        "mwb": _w_kmajor(inp["m_w_branch"][0]), "awb": _w_kmajor(inp["a_w_branch"][0]), "wout": _w_kmajor(inp["w_out"][0]),
        "wg": _w_kmajor(inp["ffn2_w_gate"][0]), "wu": _w_kmajor(inp["ffn2_w_up"][0]), "wd": _w_kmajor(inp["ffn2_w_down"][0]),
    }
    maps = []
    for c in range(NCORES):
        hd = c // 2
        m = {"A_" + n_: v for n_, v in mA[c].items()}
        m["B_convw"] = np.ascontiguousarray(np.concatenate([cwf[:, hd * 128:(hd + 1) * 128].T, cwf[:, 512 + hd * 128:512 + (hd + 1) * 128].T], axis=1))
        m["B_convb"] = np.ascontiguousarray(np.stack([cbf[hd * 128:(hd + 1) * 128], cbf[512 + hd * 128:512 + (hd + 1) * 128]], axis=1))
        m["B_cosT"], m["B_sinT"], m["B_lam4"] = cosT, sinT, lam4
        m.update({"C_" + n_: v for n_, v in sharedC.items()})
        maps.append(m)
    return maps


def kernel_unfused(**inp):
    inp = {k_: np.asarray(v) for k_, v in inp.items()}
    cores = list(range(NCORES))
    resA = run_bass_kernel_spmd(_prog("A"), prep_A(inp), core_ids=cores).results
    resB = run_bass_kernel_spmd(_prog("B"), prep_B(inp, resA), core_ids=cores).results
    resC = run_bass_kernel_spmd(_prog("C"), prep_C(inp, resA, resB), core_ids=cores).results
    out = np.concatenate([np.asarray(resC[r]["outT"]).reshape(D, TPC).T for r in range(NCORES)], axis=0)
    return np.ascontiguousarray(out.reshape(1, SEQ, D).astype(np.float32))


def kernel(**inp):
    inp = {k_: np.asarray(v) for k_, v in inp.items()}
    if "F" not in _CACHE:
        _CACHE["F"] = build_fused()
    res = run_bass_kernel_spmd(_CACHE["F"], prep_fused(inp), core_ids=list(range(NCORES))).results
    out = np.concatenate([np.asarray(res[r]["C_outT"]).reshape(D, TPC).T for r in range(NCORES)], axis=0)
    return np.ascontiguousarray(out.reshape(1, SEQ, D).astype(np.float32))
```
